# Optimizing a Trainium2 kernel written in Bass

```python
import math
import jax, jax.numpy as jnp
from jax import lax
import numpy as np

D_MODEL = 1024
BATCH = 8
SEQ = 2048
DEPTH = 2

CHUNK = 64
Q_BLOCK = 128
A_WIDTH = D_MODEL // 2
CONV_WIDTH = 31
B_WIDTH = D_MODEL // 2
LRU_BLOCKS = 8
LRU_BLOCK_DIM = B_WIDTH // LRU_BLOCKS
LRU_CONV_WIDTH = 4
LRU_C = 8.0
IN_WIDTH = 2 * A_WIDTH + 2 * B_WIDTH
DIFF_HEADS = D_MODEL // 128
DIFF_HEAD_DIM = 64
DIFF_V_DIM = 2 * DIFF_HEAD_DIM
QK_WIDTH = DIFF_HEADS * 2 * DIFF_HEAD_DIM
V_WIDTH = DIFF_HEADS * DIFF_V_DIM
ROPE_THETA = 10000.0
D_FF = ((8 * D_MODEL // 3 + 255) // 256) * 256
LN_EPS = 1e-5
DN_ALPHA = (2 * DEPTH) ** 0.25
DN_BETA = (8 * DEPTH) ** -0.25
N_EVEN = (DEPTH + 1) // 2
N_ODD = DEPTH // 2
NEG_INF = -1e30

kernel_name = 'hybrid_conv_lru_diffattn_streaming_encoder'


def layer_norm(x, g, b):
    xf = x.astype(jnp.float32)
    mu = jnp.mean(xf, axis=-1, keepdims=True)
    var = jnp.mean(jnp.square(xf - mu), axis=-1, keepdims=True)
    return ((xf - mu) * lax.rsqrt(var + LN_EPS)).astype(x.dtype) * g + b


def rms_norm(x, g):
    xf = x.astype(jnp.float32)
    return (xf * lax.rsqrt(jnp.mean(jnp.square(xf), axis=-1, keepdims=True) + LN_EPS)).astype(x.dtype) * g


def causal_depthwise_conv(x, w, b):
    k, c = w.shape
    y = lax.conv_general_dilated(x, w[:, None, :].astype(x.dtype), window_strides=(1,),
                                 padding=[(k - 1, 0)], dimension_numbers=('NWC', 'WIO', 'NWC'),
                                 feature_group_count=c)
    return y + b


def _linear_recurrence_combine(c1, c2):
    a1, u1 = c1
    a2, u2 = c2
    return a1 * a2, a2 * u1 + u2


def rg_lru(x, w_a, b_a, w_x, b_x, lam):
    bsz, seq, width = x.shape
    xb = x.reshape(bsz, seq, LRU_BLOCKS, LRU_BLOCK_DIM)
    gate_r = jax.nn.sigmoid((jnp.einsum('bsgi,gij->bsgj', xb, w_a).reshape(bsz, seq, width) + b_a).astype(jnp.float32))
    gate_i = jax.nn.sigmoid((jnp.einsum('bsgi,gij->bsgj', xb, w_x).reshape(bsz, seq, width) + b_x).astype(jnp.float32))
    log_a = -LRU_C * gate_r * jax.nn.softplus(-lam.astype(jnp.float32))
    a = jnp.exp(log_a)
    u = jnp.sqrt(-jnp.expm1(2.0 * log_a)) * (gate_i * x.astype(jnp.float32))
    _, h = lax.associative_scan(_linear_recurrence_combine, (a, u), axis=1)
    return h.astype(x.dtype)


def conv_lru_mixer(x, w_in, b_in, conv_w, conv_b, cnorm_g, cnorm_b, lru_conv_w, lru_conv_b,
                   w_a, b_a, w_x, b_x, lru_lambda, w_out):
    h = x @ w_in + b_in
    a_val = h[..., :A_WIDTH]
    a_gate = h[..., A_WIDTH:2 * A_WIDTH]
    b_gate = h[..., 2 * A_WIDTH:2 * A_WIDTH + B_WIDTH]
    b_rec = h[..., 2 * A_WIDTH + B_WIDTH:]
    ya = jax.nn.silu(layer_norm(causal_depthwise_conv(a_val * jax.nn.sigmoid(a_gate), conv_w, conv_b),
                                cnorm_g, cnorm_b))
    yb = rg_lru(causal_depthwise_conv(b_rec, lru_conv_w, lru_conv_b), w_a, b_a, w_x, b_x, lru_lambda) \
        * jax.nn.gelu(b_gate)
    return jnp.concatenate([ya, yb], axis=-1) @ w_out


def rotate_half(t):
    half = t.shape[-1] // 2
    return jnp.concatenate([-t[..., half:], t[..., :half]], axis=-1)


def diff_attention_mixer(x, w_qkv, lq1, lk1, lq2, lk2, subln_g, w_out, lambda_init):
    bsz, seq, _ = x.shape
    qkv = x @ w_qkv
    q = qkv[..., :QK_WIDTH].reshape(bsz, seq, DIFF_HEADS, 2, DIFF_HEAD_DIM)
    k = qkv[..., QK_WIDTH:2 * QK_WIDTH].reshape(bsz, seq, DIFF_HEADS, 2, DIFF_HEAD_DIM)
    v = qkv[..., 2 * QK_WIDTH:].reshape(bsz, seq, DIFF_HEADS, DIFF_V_DIM)
    pos = jnp.arange(seq, dtype=jnp.float32)
    inv_freq = ROPE_THETA ** (-jnp.arange(0, DIFF_HEAD_DIM, 2, dtype=jnp.float32) / DIFF_HEAD_DIM)
    ang = pos[:, None] * inv_freq[None, :]
    ang = jnp.concatenate([ang, ang], axis=-1)
    cos = jnp.cos(ang).astype(x.dtype)[:, None, None, :]
    sin = jnp.sin(ang).astype(x.dtype)[:, None, None, :]
    q = (q * cos + rotate_half(q) * sin) * (DIFF_HEAD_DIM ** -0.5)
    k = k * cos + rotate_half(k) * sin
    q = q.transpose(0, 2, 3, 1, 4)
    k = k.transpose(0, 2, 3, 1, 4)
    v = v.transpose(0, 2, 1, 3)
    lam = (jnp.exp(jnp.sum(lq1.astype(jnp.float32) * lk1.astype(jnp.float32)))
           - jnp.exp(jnp.sum(lq2.astype(jnp.float32) * lk2.astype(jnp.float32))) + lambda_init)
    chunk_id = jnp.arange(seq) // CHUNK
    outs = []
    for qb in range(seq // Q_BLOCK):
        s0, s1 = qb * Q_BLOCK, (qb + 1) * Q_BLOCK
        scores = jnp.einsum('bhmqd,bhmkd->bhmqk', q[:, :, :, s0:s1], k[:, :, :, :s1],
                            preferred_element_type=jnp.float32)
        mask = chunk_id[s0:s1, None] >= chunk_id[None, :s1]
        p = jax.nn.softmax(jnp.where(mask, scores, NEG_INF), axis=-1)
        attn = p[:, :, 0] - lam * p[:, :, 1]
        outs.append(jnp.einsum('bhqk,bhkd->bhqd', attn.astype(v.dtype), v[:, :, :s1]))
    o = jnp.concatenate(outs, axis=2)
    o = rms_norm(o, subln_g) * (1.0 - lambda_init)
    o = o.transpose(0, 2, 1, 3).reshape(bsz, seq, V_WIDTH)
    return o @ w_out


def swiglu(x, w_gate, w_up, w_down):
    return (jax.nn.silu(x @ w_gate) * (x @ w_up)) @ w_down


def setup_inputs(seed: int = 0) -> dict:
    key = jax.random.key(seed)
    ks = iter(jax.random.split(key, 40))
    f32 = jnp.float32

    def nrm(shape, scale):
        return scale * jax.random.normal(next(ks), shape, f32)

    def gain(shape):
        return 1.0 + nrm(shape, 0.02)

    a_pow = jax.random.uniform(next(ks), (N_EVEN, B_WIDTH), f32, minval=0.9, maxval=0.999)
    a0 = a_pow ** (1.0 / LRU_C)
    lru_lambda = jnp.log(a0) - jnp.log1p(-a0)
    return {
        'x': nrm((BATCH, SEQ, D_MODEL), 1.0),
        'even_w_in': nrm((N_EVEN, D_MODEL, IN_WIDTH), D_MODEL ** -0.5),
        'even_b_in': nrm((N_EVEN, IN_WIDTH), 0.01),
        'even_conv_w': nrm((N_EVEN, CONV_WIDTH, A_WIDTH), CONV_WIDTH ** -0.5),
        'even_conv_b': nrm((N_EVEN, A_WIDTH), 0.01),
        'even_cnorm_g': gain((N_EVEN, A_WIDTH)),
        'even_cnorm_b': nrm((N_EVEN, A_WIDTH), 0.01),
        'even_lru_conv_w': nrm((N_EVEN, LRU_CONV_WIDTH, B_WIDTH), LRU_CONV_WIDTH ** -0.5),
        'even_lru_conv_b': nrm((N_EVEN, B_WIDTH), 0.01),
        'even_w_a': nrm((N_EVEN, LRU_BLOCKS, LRU_BLOCK_DIM, LRU_BLOCK_DIM), LRU_BLOCK_DIM ** -0.5),
        'even_b_a': nrm((N_EVEN, B_WIDTH), 0.01),
        'even_w_x': nrm((N_EVEN, LRU_BLOCKS, LRU_BLOCK_DIM, LRU_BLOCK_DIM), LRU_BLOCK_DIM ** -0.5),
        'even_b_x': nrm((N_EVEN, B_WIDTH), 0.01),
        'even_lru_lambda': lru_lambda,
        'even_w_out': nrm((N_EVEN, A_WIDTH + B_WIDTH, D_MODEL), DN_BETA * (A_WIDTH + B_WIDTH) ** -0.5),
        'odd_w_qkv': nrm((N_ODD, D_MODEL, 2 * QK_WIDTH + V_WIDTH), D_MODEL ** -0.5),
        'odd_lambda_q1': nrm((N_ODD, DIFF_HEAD_DIM), 0.1),
        'odd_lambda_k1': nrm((N_ODD, DIFF_HEAD_DIM), 0.1),
        'odd_lambda_q2': nrm((N_ODD, DIFF_HEAD_DIM), 0.1),
        'odd_lambda_k2': nrm((N_ODD, DIFF_HEAD_DIM), 0.1),
        'odd_subln_g': gain((N_ODD, DIFF_V_DIM)),
        'odd_w_out': nrm((N_ODD, V_WIDTH, D_MODEL), DN_BETA * V_WIDTH ** -0.5),
        'mix_ln_g': gain((DEPTH, D_MODEL)),
        'mix_ln_b': nrm((DEPTH, D_MODEL), 0.01),
        'ffn_w_gate': nrm((DEPTH, D_MODEL, D_FF), D_MODEL ** -0.5),
        'ffn_w_up': nrm((DEPTH, D_MODEL, D_FF), D_MODEL ** -0.5),
        'ffn_w_down': nrm((DEPTH, D_FF, D_MODEL), DN_BETA * D_FF ** -0.5),
        'ffn_ln_g': gain((DEPTH, D_MODEL)),
        'ffn_ln_b': nrm((DEPTH, D_MODEL), 0.01),
    }


def reference(x, even_w_in, even_b_in, even_conv_w, even_conv_b, even_cnorm_g, even_cnorm_b,
              even_lru_conv_w, even_lru_conv_b, even_w_a, even_b_a, even_w_x, even_b_x,
              even_lru_lambda, even_w_out, odd_w_qkv, odd_lambda_q1, odd_lambda_k1,
              odd_lambda_q2, odd_lambda_k2, odd_subln_g, odd_w_out, mix_ln_g, mix_ln_b,
              ffn_w_gate, ffn_w_up, ffn_w_down, ffn_ln_g, ffn_ln_b):
    for layer in range(DEPTH):
        if layer % 2 == 0:
            e = layer // 2
            y = conv_lru_mixer(x, even_w_in[e], even_b_in[e], even_conv_w[e], even_conv_b[e],
                               even_cnorm_g[e], even_cnorm_b[e], even_lru_conv_w[e], even_lru_conv_b[e],
                               even_w_a[e], even_b_a[e], even_w_x[e], even_b_x[e],
                               even_lru_lambda[e], even_w_out[e])
        else:
            o = layer // 2
            lambda_init = 0.8 - 0.6 * math.exp(-0.3 * layer)
            y = diff_attention_mixer(x, odd_w_qkv[o], odd_lambda_q1[o], odd_lambda_k1[o],
                                     odd_lambda_q2[o], odd_lambda_k2[o], odd_subln_g[o],
                                     odd_w_out[o], lambda_init)
        x = layer_norm(DN_ALPHA * x + y, mix_ln_g[layer], mix_ln_b[layer])
        x = layer_norm(DN_ALPHA * x + swiglu(x, ffn_w_gate[layer], ffn_w_up[layer], ffn_w_down[layer]),
                       ffn_ln_g[layer], ffn_ln_b[layer])
    return x
```

```python
import math
from contextlib import ExitStack

import numpy as np
import concourse.bass as bass
import concourse.mybir as mybir
from concourse.bass_utils import run_bass_kernel_spmd

F32, BF16 = mybir.dt.float32, mybir.dt.bfloat16
AF = mybir.ActivationFunctionType
ALU = mybir.AluOpType

P = 128
S = 2048
D = 1024
NC_ = 8
TB = 512
NTB = S // TB
DFF = 2816
NFC = DFF // P
LN_EPS = 1e-5
DN_ALPHA = 4 ** 0.25
NSLOT = 3
SLAB = 4096

PC = {}
_o = 0
for _n, _k in [("b_in", 16), ("conv_w", 124), ("conv_b", 4), ("cn_g", 4), ("cn_b", 4), ("lconv_w", 16),
               ("lconv_b", 4), ("b_a", 4), ("b_x", 4), ("lam", 4), ("mix_g", 16), ("mix_b", 16),
               ("ffn_g", 16), ("ffn_b", 16)]:
    PC[_n] = _o
    _o += _k
NPC = _o


def _cols(v):
    v = np.asarray(v, np.float32).reshape(-1, P)
    return np.ascontiguousarray(v.T)


USED_INPUTS = []


class Tile:
    __slots__ = ("name", "w", "r", "excl")

    def __init__(self, name, excl=False):
        self.name = name
        self.w = None
        self.r = {}
        self.excl = excl


def _flat(x):
    out = []
    for t in x:
        if isinstance(t, (list, tuple)):
            out.extend(_flat(t))
        elif t is not None:
            out.append(t)
    return out


class KB:
    def __init__(self, nc, es):
        self.nc = nc
        self.es = es
        self.engs = {"pe": nc.tensor, "act": nc.scalar, "dve": nc.vector, "pool": nc.gpsimd, "sp": nc.sync}
        self.sems = {}
        self.cnt = {}
        self.wm = {e: {} for e in self.engs}
        self.last = {e: None for e in self.engs}
        for e in ("pe", "act", "dve", "pool"):
            self.sem(e)
        self.banks = []
        self.bank_i = 0
        self.n_ins = 0

    def sem(self, key):
        if key not in self.sems:
            self.sems[key] = self.es.enter_context(self.nc.semaphore("s_" + key))
            self.cnt[key] = 0
        return self.sems[key]

    def flush(self, e):
        ins = self.last[e]
        assert ins is not None, e
        ins.then_inc(self.sems[e], 1)
        self.cnt[e] += 1
        self.last[e] = None

    def _deps(self, eng, reads, writes):
        deps = {}

        def add(k, v, same_ok):
            if k == eng and not same_ok:
                return
            if deps.get(k, 0) < v:
                deps[k] = v
        for t in reads:
            if t.w is not None:
                add(t.w[0], t.w[1], eng != "pe")
            if t.excl:
                for k, v in t.r.items():
                    add(k, v, False)
        for t in writes:
            if t.w is not None:
                add(t.w[0], t.w[1], False)
            for k, v in t.r.items():
                add(k, v, False)
        for k, v in deps.items():
            if v > self.cnt[k]:
                self.flush(k)
                assert v <= self.cnt[k]
            if self.wm[eng].get(k, 0) < v:
                self.engs[eng].wait_ge(self.sems[k], v)
                self.wm[eng][k] = v

    def op(self, eng, fn, reads=(), writes=(), inc=True):
        reads = _flat(reads)
        writes = _flat(writes)
        self._deps(eng, reads, writes)
        ins = fn()
        self.n_ins += 1
        if inc:
            ins.then_inc(self.sems[eng], 1)
            self.cnt[eng] += 1
            ev = self.cnt[eng]
            self.last[eng] = None
        else:
            ev = self.cnt[eng] + 1
            self.last[eng] = ins
        for t in reads:
            if t.r.get(eng, 0) < ev:
                t.r[eng] = ev
        for t in writes:
            t.w = (eng, ev)
            t.r = {}
        return ins

    def dma(self, q, out, in_, reads=(), writes=(), semkey=None, clear=False):
        reads = _flat(reads)
        writes = _flat(writes)
        self.sem(semkey)
        self._deps(q, reads, writes)
        if clear and self.cnt[semkey] > 0:
            self.engs[q].wait_ge(self.sems[semkey], self.cnt[semkey])
            self.engs[q].sem_clear(self.sems[semkey])
            self.cnt[semkey] = 0
            for e in self.wm:
                self.wm[e].pop(semkey, None)
        ins = self.engs[q].dma_start(out=out, in_=in_)
        ins.then_inc(self.sems[semkey], 16)
        self.cnt[semkey] += 16
        ev = self.cnt[semkey]
        for t in reads:
            t.r[semkey] = ev
        for t in writes:
            t.w = (semkey, ev)
            t.r = {}

    def barrier(self):
        ce = ("pe", "act", "dve", "pool")
        for e in ce:
            if self.last[e] is not None:
                self.flush(e)
        for e in ("pe", "act", "dve", "pool", "sp"):
            for k in ce:
                if k != e and self.wm[e].get(k, 0) < self.cnt[k]:
                    self.engs[e].wait_ge(self.sems[k], self.cnt[k])
                    self.wm[e][k] = self.cnt[k]

    def bank(self):
        b = self.banks[self.bank_i % 8]
        self.bank_i += 1
        return b

    def sb(self, name, shape, dt, es=None):
        self.n_sb = getattr(self, "n_sb", 0) + 1
        return (es or self.es).enter_context(self.nc.sbuf_tensor(f"sb{self.n_sb}_{name}", shape, dt))


class Ring:
    def __init__(self, kb, n):
        self.kb = kb
        self.n = n
        self.slots = [kb.sb(f"ring{i}", [P, SLAB], BF16) for i in range(n)]
        self.tiles = [Tile(f"ring{i}") for i in range(n)]
        self.plan = []
        self.issued = 0
        self.released = set()
        self.gen = [0] * n

    def add(self, *parts):
        self.plan.append(parts)
        return len(self.plan) - 1

    def _views(self, i):
        out = []
        off = 0
        for _, a, b in self.plan[i]:
            out.append(self.slots[i % self.n][:, off:off + a * b].rearrange("p (a b) -> p a b", a=a))
            off += a * b
        assert off <= SLAB
        return out

    def pump(self):
        while self.issued < len(self.plan) and (self.issued < self.n or (self.issued - self.n) in self.released):
            i = self.issued
            s = i % self.n
            need = 16 * len(self.plan[i])
            key = f"ring{s}_{self.gen[s]}"
            if self.kb.cnt.get(key, 0) + need > 128:
                self.gen[s] += 1
                key = f"ring{s}_{self.gen[s]}"
            for (ap3, a, b), v in zip(self.plan[i], self._views(i)):
                self.kb.dma("pool", v, ap3, writes=[self.tiles[s]], semkey=key)
            self.issued += 1

    def use(self, i):
        self.pump()
        assert i < self.issued, ("slab not issued", i, self.issued)
        v = self._views(i)
        return (v[0] if len(v) == 1 else v), self.tiles[i % self.n]

    def release(self, i):
        self.released.add(i)
        self.pump()


def build_nc(layers=(0, 1), dbg=False, stop=None):
    nc = bass.Bass("TRN2", target_bir_lowering=False)
    dcache = {}

    def din(n, s):
        if n not in dcache:
            dcache[n] = nc.dram_tensor(n, list(s), F32, kind="ExternalInput").ap()
        return dcache[n]
    USED_INPUTS.clear()
    xT_d = din("xT", (D, S))
    pcol_d = din("pcol", (P, NPC))
    ident_d = din("ident", (P, P))
    gbd_d = din("gbd", (8, P, P))
    has_ffn = stop not in ("init", "s1", "s2", "mixer")
    if 0 in layers:
        w_in_d = din("w_in", (D, 2048))
        w_out0_d = din("w_out0", (D, D))
    if 1 in layers:
        perm_d = din("perm", (P, P))
        cos_d = din("cosT", (P, S))
        sin_d = din("sinT", (P, S))
        subg_d = din("subg", (P, P))
        lamv_d = din("lamv", (P, 4 * 64))
        w_qkv_d = din("w_qkv", (D, 3072))
        w_out1_d = din("w_out1", (D, D))
    wg_d, wu_d, wd_d = {}, {}, {}
    if has_ffn:
        for l in layers:
            wg_d[l] = din(f"wg{l}", (D, DFF))
            wu_d[l] = din(f"wu{l}", (D, DFF))
            wd_d[l] = din(f"wd{l}", (DFF, D))
    USED_INPUTS.extend(dcache)
    yT_d = nc.dram_tensor("yT", [D, S], F32, kind="ExternalOutput").ap()

    with ExitStack() as es:
        kb = KB(nc, es)
        op, dma = kb.op, kb.dma
        pe, act, dve, pool = nc.tensor, nc.scalar, nc.vector, nc.gpsimd
        for i in range(8):
            kb.banks.append((es.enter_context(nc.psum_tensor(f"ps{i}", [P, TB], F32)), Tile(f"ps{i}", excl=True)))

        xf = kb.sb("xf", [P, NC_, S], F32)
        xb = kb.sb("xb", [P, NC_, S], BF16)
        xf_t = [[Tile(f"xf{c}_{t}") for t in range(NTB)] for c in range(NC_)]
        xb_t = [[Tile(f"xb{c}_{t}") for t in range(NTB)] for c in range(NC_)]
        ring = Ring(kb, NSLOT)
        pcol = kb.sb("pcol", [P, NPC], F32)
        ident_f = kb.sb("ident_f", [P, P], F32)
        ident_b = kb.sb("ident_b", [P, P], BF16)
        ones1024 = kb.sb("ones1024", [P, P], BF16)
        ones512 = kb.sb("ones512", [P, P], BF16)
        gbd = kb.sb("gbd", [P, 8, P], BF16)
        nls = kb.sb("nls", [P, 8], F32)
        cst = Tile("consts")

        def pc(name, i=0):
            o = PC[name] + i
            return pcol[:, o:o + 1]

        def tbs(t):
            return slice(t * TB, (t + 1) * TB)

        def colslab(w, c0, ncols):
            return w.rearrange("(k p) c -> p k c", p=P)[:, :, c0:c0 + ncols], 8, ncols

        def rowslab(w, r0, nk, c0):
            return w[r0 * P:(r0 + nk) * P, c0:c0 + 512].rearrange("(k p) c -> p k c", p=P), nk, 512

        plan = {}
        for l in layers:
            if l == 0:
                plan["in"] = [ring.add(colslab(w_in_d, c0, 512)) for c0 in (1024, 1536, 0, 512)]
                plan["out0"] = [ring.add(colslab(w_out0_d, c0, 512)) for c0 in (0, 512)]
            else:
                plan["qkv"] = [[ring.add(colslab(w_qkv_d, base + g * 512, 512)) for base in (0, 1024, 2048)]
                               for g in range(2)]
                plan["out1"] = [ring.add(colslab(w_out1_d, c0, 512)) for c0 in (0, 512)]
            for th in range(2 if has_ffn else 0):
                gu = [ring.add(colslab(wg_d[l], j * 256, 256), colslab(wu_d[l], j * 256, 256)) for j in range(11)]
                dn = []
                for ch in range(2):
                    for r0, nk in ((0, 8), (8, 8), (16, 6)):
                        dn.append(ring.add(rowslab(wd_d[l], r0, nk, ch * 512)))
                plan[("ffn", l, th)] = (gu, dn)

        dma("sp", pcol[:], pcol_d, writes=[cst], semkey="cst")
        dma("sp", ident_f[:], ident_d, writes=[cst], semkey="cst")
        for c in range(NC_):
            dma("sp", xf[:, c, :], xT_d[c * P:(c + 1) * P, :], writes=xf_t[c], semkey=f"xin{c % 2}")
        for c in range(NC_):
            dma("pool", xb[:, c, :], xT_d[c * P:(c + 1) * P, :], writes=xb_t[c], semkey=f"xinb{c % 2}")
        for c in range(NC_):
            for t in xf_t[c]:
                t.w = (f"xin{c % 2}", kb.cnt[f"xin{c % 2}"])
            for t in xb_t[c]:
                t.w = (f"xinb{c % 2}", kb.cnt[f"xinb{c % 2}"])
        gbd_t = Tile("gbd")
        dma("pool", gbd[:], gbd_d.rearrange("g p m -> p g m"), writes=[gbd_t], semkey="gbdl")
        cst.w = ("cst", kb.cnt["cst"])
        ring.pump()
        op("dve", lambda: dve.memset(ones1024[:], 1.0 / 1024), writes=[cst])
        op("dve", lambda: dve.memset(ones512[:], 1.0 / 512), writes=[cst])
        op("dve", lambda: dve.tensor_copy(out=ident_b[:], in_=ident_f[:]), reads=[cst], writes=[cst])
        op("act", lambda: act.activation(out=nls[:, 0:4], in_=pcol[:, PC["lam"]:PC["lam"] + 4], func=AF.Exp, scale=-1.0),
           reads=[cst], writes=[cst])
        op("act", lambda: act.activation(out=nls[:, 0:4], in_=nls[:, 0:4], func=AF.Ln, bias=1.0, scale=1.0),
           reads=[cst], writes=[cst])
        op("dve", lambda: dve.tensor_scalar(out=nls[:, 4:8], in0=nls[:, 0:4], scalar1=-16.0, scalar2=None, op0=ALU.mult),
           reads=[cst], writes=[cst])
        op("dve", lambda: dve.tensor_scalar(out=nls[:, 0:4], in0=nls[:, 0:4], scalar1=-8.0, scalar2=None, op0=ALU.mult),
           reads=[cst], writes=[cst])

        def mm_group(bank, pairs, ncols=TB, reads=()):
            bt, btile = bank
            n = len(pairs)
            for i, (lh, rh) in enumerate(pairs):
                op("pe", lambda lh=lh, rh=rh, i=i: pe.matmul(bt[:, 0:ncols], lh, rh, start=(i == 0), stop=(i == n - 1)),
                   reads=reads, writes=[btile], inc=(i == n - 1))

        def layer_norm_block(tb, gname, bname, goff, tmp):
            zb, sq, m2, rstd, tt = tmp
            xt = [xf_t[c][tb] for c in range(NC_)]
            op("pool", lambda: pool.tensor_copy(out=zb[0][:], in_=xf[:, :, tbs(tb)]), reads=xt, writes=[zb[1]])
            op("act", lambda: act.activation(out=sq[0][:], in_=xf[:, :, tbs(tb)], func=AF.Square), reads=xt, writes=[sq[1]])
            s1 = kb.bank()
            s2 = kb.bank()
            mm_group(s1, [(ones1024[:], zb[0][:, c, :]) for c in range(NC_)], reads=[cst, zb[1]])
            mm_group(s2, [(ones1024[:], sq[0][:, c, :]) for c in range(NC_)], reads=[cst, sq[1]])
            op("act", lambda: act.activation(out=m2[0][:], in_=s1[0][:], func=AF.Square), reads=[s1[1]], writes=[m2[1]])
            op("dve", lambda: dve.tensor_tensor(out=m2[0][:], in0=s2[0][:], in1=m2[0][:], op=ALU.subtract),
               reads=[s2[1], m2[1]], writes=[m2[1]])
            op("act", lambda: act.activation(out=m2[0][:], in_=m2[0][:], func=AF.Sqrt, bias=LN_EPS, scale=1.0),
               reads=[m2[1]], writes=[m2[1]])
            op("dve", lambda: dve.reciprocal(out=rstd[0][:], in_=m2[0][:]), reads=[m2[1]], writes=[rstd[1]])
            for c in range(NC_):
                t0, tt0 = tt[c % 2]
                op("dve", lambda c=c, t0=t0: dve.tensor_tensor(out=t0[:], in0=xf[:, c, tbs(tb)], in1=s1[0][:], op=ALU.subtract),
                   reads=[xf_t[c][tb], s1[1]], writes=[tt0])
                op("dve", lambda t0=t0: dve.tensor_tensor(out=t0[:], in0=t0[:], in1=rstd[0][:], op=ALU.mult),
                   reads=[tt0, rstd[1]], writes=[tt0])
                op("act", lambda c=c, t0=t0: act.activation(out=xf[:, c, tbs(tb)], in_=t0[:], func=AF.Identity,
                                                             bias=pc(bname, goff + c), scale=pc(gname, goff + c)),
                   reads=[tt0, cst], writes=[xf_t[c][tb]])
                op("act", lambda c=c, t0=t0: act.activation(out=xb[:, c, tbs(tb)], in_=t0[:], func=AF.Identity,
                                                             bias=pc(bname, goff + c), scale=pc(gname, goff + c)),
                   reads=[tt0, cst], writes=[xb_t[c][tb]])

        def ln_tmps(les):
            zb = (kb.sb("ln_zb", [P, NC_, TB], BF16, les), Tile("ln_zb"))
            sq = (kb.sb("ln_sq", [P, NC_, TB], BF16, les), Tile("ln_sq"))
            m2 = (kb.sb("ln_m2", [P, TB], F32, les), Tile("ln_m2"))
            rstd = (kb.sb("ln_rstd", [P, TB], F32, les), Tile("ln_rstd"))
            tt = [(kb.sb(f"ln_tt{i}", [P, TB], F32, les), Tile(f"ln_tt{i}")) for i in range(2)]
            return zb, sq, m2, rstd, tt

        def layer0_mixer():
            s_bg, s_br, s_av, s_ag = plan["in"]
            with ExitStack() as les:
                cat_hi = kb.sb("cat_hi", [P, 4, S], BF16, les)
                cat_t = [[Tile(f"cat{c}_{t}") for t in range(NTB)] for c in range(4)]
                with ExitStack() as s1es:
                    gx = (kb.sb("gx", [P, S], F32, s1es), Tile("gx"))
                    tA = (kb.sb("tA", [P, S], F32, s1es), Tile("tA"))
                    brp = (kb.sb("brp", [P, 3 + S], BF16, s1es), Tile("brp"))
                    dg4 = (kb.sb("dg4", [P, 4, P], BF16, s1es), Tile("dg4"))
                    xc = (kb.sb("xc", [P, S], F32, s1es), Tile("xc"))
                    xcb = (kb.sb("xcb", [P, S], BF16, s1es), Tile("xcb"))
                    rr = (kb.sb("rr", [P, S], F32, s1es), Tile("rr"))
                    gi = (kb.sb("gi", [P, S], F32, s1es), Tile("gi"))
                    hh = (kb.sb("hh", [P, S], F32, s1es), Tile("hh"))
                    op("pool", lambda: pool.memset(brp[0][:, 0:3], 0.0), writes=[brp[1]])
                    wbg, wbg_t = ring.use(s_bg)
                    wbr, wbr_t = ring.use(s_br)
                    for c in range(4):
                        for tb in range(NTB):
                            bk = kb.bank()
                            mm_group(bk, [(wbg[:, k, c * P:(c + 1) * P], xb[:, k, tbs(tb)]) for k in range(NC_)],
                                     reads=[wbg_t] + [xb_t[k][tb] for k in range(NC_)])
                            op("act", lambda bk=bk, tb=tb, c=c: act.activation(out=gx[0][:, tbs(tb)], in_=bk[0][:], func=AF.Identity,
                                                                            bias=pc("b_in", 8 + c), scale=1.0),
                               reads=[bk[1], cst], writes=[gx[1]])
                        op("act", lambda: act.activation(out=tA[0][:], in_=gx[0][:], func=AF.Square), reads=[gx[1]], writes=[tA[1]])
                        op("dve", lambda: dve.tensor_scalar(out=tA[0][:], in0=tA[0][:], scalar1=0.044715, scalar2=1.0,
                                                            op0=ALU.mult, op1=ALU.add), reads=[tA[1]], writes=[tA[1]])
                        op("dve", lambda: dve.tensor_tensor(out=tA[0][:], in0=tA[0][:], in1=gx[0][:], op=ALU.mult),
                           reads=[tA[1], gx[1]], writes=[tA[1]])
                        op("act", lambda: act.activation(out=tA[0][:], in_=tA[0][:], func=AF.Sigmoid, scale=2.0 * math.sqrt(2.0 / math.pi)),
                           reads=[tA[1]], writes=[tA[1]])
                        op("pool", lambda: pool.tensor_tensor(out=gx[0][:], in0=gx[0][:], in1=tA[0][:], op=ALU.mult),
                           reads=[tA[1], gx[1]], writes=[gx[1]])
                        for tb in range(NTB):
                            bk = kb.bank()
                            mm_group(bk, [(wbr[:, k, c * P:(c + 1) * P], xb[:, k, tbs(tb)]) for k in range(NC_)],
                                     reads=[wbr_t] + [xb_t[k][tb] for k in range(NC_)])
                            op("act", lambda bk=bk, tb=tb, c=c: act.activation(out=brp[0][:, 3 + tb * TB:3 + (tb + 1) * TB], in_=bk[0][:],
                                                                            func=AF.Identity, bias=pc("b_in", 12 + c), scale=1.0),
                               reads=[bk[1], cst], writes=[brp[1]])
                        for j in range(4):
                            op("pool", lambda j=j, c=c: pool.tensor_scalar(out=dg4[0][:, j, :], in0=ident_f[:], scalar1=pc("lconv_w", j * 4 + c),
                                                                         scalar2=0.0, op0=ALU.mult, op1=ALU.add), reads=[cst], writes=[dg4[1]])
                        for tb in range(NTB):
                            bk = kb.bank()
                            mm_group(bk, [(dg4[0][:, j, :], brp[0][:, tb * TB + j:tb * TB + j + TB]) for j in range(4)],
                                     reads=[dg4[1], brp[1]])
                            op("act", lambda bk=bk, tb=tb, c=c: act.activation(out=xc[0][:, tbs(tb)], in_=bk[0][:], func=AF.Identity,
                                                                            bias=pc("lconv_b", c), scale=1.0),
                               reads=[bk[1], cst], writes=[xc[1]])
                            op("dve", lambda bk=bk, tb=tb, c=c: dve.tensor_scalar(out=xcb[0][:, tbs(tb)], in0=bk[0][:], scalar1=pc("lconv_b", c),
                                                                               scalar2=None, op0=ALU.add),
                               reads=[bk[1], cst], writes=[xcb[1]])
                        for tb in range(NTB):
                            bk = kb.bank()
                            mm_group(bk, [(gbd[:, c, :], xcb[0][:, tbs(tb)])], reads=[gbd_t, xcb[1]])
                            op("act", lambda bk=bk, tb=tb, c=c: act.activation(out=rr[0][:, tbs(tb)], in_=bk[0][:], func=AF.Sigmoid,
                                                                            bias=pc("b_a", c), scale=1.0),
                               reads=[bk[1], cst], writes=[rr[1]])
                            bk2 = kb.bank()
                            mm_group(bk2, [(gbd[:, 4 + c, :], xcb[0][:, tbs(tb)])], reads=[gbd_t, xcb[1]])
                            op("act", lambda bk2=bk2, tb=tb, c=c: act.activation(out=gi[0][:, tbs(tb)], in_=bk2[0][:], func=AF.Sigmoid,
                                                                              bias=pc("b_x", c), scale=1.0),
                               reads=[bk2[1], cst], writes=[gi[1]])
                        op("act", lambda c=c: act.activation(out=tA[0][:], in_=rr[0][:], func=AF.Exp, scale=nls[:, 4 + c:5 + c]),
                           reads=[rr[1], cst, tA[1]], writes=[tA[1]])
                        op("act", lambda c=c: act.activation(out=rr[0][:], in_=rr[0][:], func=AF.Exp, scale=nls[:, c:c + 1]),
                           reads=[rr[1], cst], writes=[rr[1]])
                        op("act", lambda: act.activation(out=tA[0][:], in_=tA[0][:], func=AF.Sqrt, bias=1.0, scale=-1.0),
                           reads=[tA[1]], writes=[tA[1]])
                        op("dve", lambda: dve.tensor_tensor(out=gi[0][:], in0=gi[0][:], in1=xc[0][:], op=ALU.mult),
                           reads=[gi[1], xc[1]], writes=[gi[1]])
                        op("pool", lambda: pool.tensor_tensor(out=gi[0][:], in0=gi[0][:], in1=tA[0][:], op=ALU.mult),
                           reads=[gi[1], tA[1]], writes=[gi[1]])
                        op("dve", lambda: dve.tensor_tensor_scan(out=hh[0][:], data0=rr[0][:], data1=gi[0][:], initial=0.0,
                                                                 op0=ALU.mult, op1=ALU.add),
                           reads=[rr[1], gi[1]], writes=[hh[1]])
                        op("dve", lambda c=c: dve.tensor_tensor(out=cat_hi[:, c, :], in0=hh[0][:], in1=gx[0][:], op=ALU.mult),
                           reads=[hh[1], gx[1]], writes=cat_t[c])
                    ring.release(s_bg)
                    ring.release(s_br)
                kb.barrier()
                if stop == "s1":
                    return
                with ExitStack() as s2es:
                    vv = kb.sb("vv", [P, 4, S], F32, s2es)
                    vv_t = [[Tile(f"vv{c}_{t}") for t in range(NTB)] for c in range(4)]
                    sig = (kb.sb("sig", [P, S], BF16, s2es), Tile("sig"))
                    up = (kb.sb("up", [P, 30 + S], BF16, s2es), Tile("up"))
                    dg = (kb.sb("dg31", [P, 31, P], BF16, s2es), Tile("dg31"))
                    op("pool", lambda: pool.memset(up[0][:, 0:30], 0.0), writes=[up[1]])
                    wav, wav_t = ring.use(s_av)
                    wag, wag_t = ring.use(s_ag)
                    for c in range(4):
                        for tb in range(NTB):
                            bk = kb.bank()
                            mm_group(bk, [(wag[:, k, c * P:(c + 1) * P], xb[:, k, tbs(tb)]) for k in range(NC_)],
                                     reads=[wag_t] + [xb_t[k][tb] for k in range(NC_)])
                            op("act", lambda bk=bk, tb=tb, c=c: act.activation(out=sig[0][:, tbs(tb)], in_=bk[0][:], func=AF.Sigmoid,
                                                                            bias=pc("b_in", 4 + c), scale=1.0),
                               reads=[bk[1], cst], writes=[sig[1]])
                        for tb in range(NTB):
                            bk = kb.bank()
                            mm_group(bk, [(wav[:, k, c * P:(c + 1) * P], xb[:, k, tbs(tb)]) for k in range(NC_)],
                                     reads=[wav_t] + [xb_t[k][tb] for k in range(NC_)])
                            op("dve", lambda bk=bk, tb=tb, c=c: dve.scalar_tensor_tensor(out=up[0][:, 30 + tb * TB:30 + (tb + 1) * TB], in0=bk[0][:],
                                                                                      scalar=pc("b_in", c), in1=sig[0][:, tbs(tb)],
                                                                                      op0=ALU.add, op1=ALU.mult),
                               reads=[bk[1], cst, sig[1]], writes=[up[1]])
                        for j in range(31):
                            op("pool", lambda j=j, c=c: pool.tensor_scalar(out=dg[0][:, j, :], in0=ident_f[:], scalar1=pc("conv_w", j * 4 + c),
                                                                         scalar2=0.0, op0=ALU.mult, op1=ALU.add), reads=[cst], writes=[dg[1]])
                        for tb in range(NTB):
                            bk = kb.bank()
                            mm_group(bk, [(dg[0][:, j, :], up[0][:, tb * TB + j:tb * TB + j + TB]) for j in range(31)],
                                     reads=[dg[1], up[1]])
                            op("act", lambda bk=bk, tb=tb, c=c: act.activation(out=vv[:, c, tbs(tb)], in_=bk[0][:], func=AF.Identity,
                                                                            bias=pc("conv_b", c), scale=1.0),
                               reads=[bk[1], cst], writes=[vv_t[c][tb]])
                    ring.release(s_av)
                    ring.release(s_ag)
                    vb = (kb.sb("vb", [P, 4, TB], BF16, s2es), Tile("vb"))
                    vq = (kb.sb("vq", [P, 4, TB], BF16, s2es), Tile("vq"))
                    m2 = (kb.sb("cm2", [P, TB], F32, s2es), Tile("cm2"))
                    rstd = (kb.sb("crstd", [P, TB], F32, s2es), Tile("crstd"))
                    tt = [(kb.sb(f"ctt{i}", [P, TB], F32, s2es), Tile(f"ctt{i}")) for i in range(2)]
                    for tb in range(NTB):
                        vt = [vv_t[c][tb] for c in range(4)]
                        op("pool", lambda tb=tb: pool.tensor_copy(out=vb[0][:], in_=vv[:, :, tbs(tb)]), reads=vt, writes=[vb[1]])
                        op("act", lambda tb=tb: act.activation(out=vq[0][:], in_=vv[:, :, tbs(tb)], func=AF.Square), reads=vt, writes=[vq[1]])
                        s1 = kb.bank()
                        s2 = kb.bank()
                        mm_group(s1, [(ones512[:], vb[0][:, c, :]) for c in range(4)], reads=[cst, vb[1]])
                        mm_group(s2, [(ones512[:], vq[0][:, c, :]) for c in range(4)], reads=[cst, vq[1]])
                        op("act", lambda s1=s1: act.activation(out=m2[0][:], in_=s1[0][:], func=AF.Square), reads=[s1[1]], writes=[m2[1]])
                        op("dve", lambda s2=s2: dve.tensor_tensor(out=m2[0][:], in0=s2[0][:], in1=m2[0][:], op=ALU.subtract),
                           reads=[s2[1], m2[1]], writes=[m2[1]])
                        op("act", lambda: act.activation(out=m2[0][:], in_=m2[0][:], func=AF.Sqrt, bias=LN_EPS, scale=1.0),
                           reads=[m2[1]], writes=[m2[1]])
                        op("dve", lambda: dve.reciprocal(out=rstd[0][:], in_=m2[0][:]), reads=[m2[1]], writes=[rstd[1]])
                        for c in range(4):
                            t0, tt0 = tt[c % 2]
                            op("dve", lambda c=c, t0=t0, s1=s1, tb=tb: dve.tensor_tensor(out=t0[:], in0=vv[:, c, tbs(tb)], in1=s1[0][:], op=ALU.subtract),
                               reads=[vv_t[c][tb], s1[1]], writes=[tt0])
                            op("dve", lambda t0=t0: dve.tensor_tensor(out=t0[:], in0=t0[:], in1=rstd[0][:], op=ALU.mult),
                               reads=[tt0, rstd[1]], writes=[tt0])
                            op("act", lambda c=c, t0=t0, tb=tb: act.activation(out=xb[:, c, tbs(tb)], in_=t0[:], func=AF.Silu,
                                                                            bias=pc("cn_b", c), scale=pc("cn_g", c)),
                               reads=[tt0, cst], writes=[xb_t[c][tb]])
                kb.barrier()
                if stop == "s2":
                    return
                with ExitStack() as oes:
                    tmp = ln_tmps(oes)
                    w0, w0_t = ring.use(plan["out0"][0])
                    w1, w1_t = ring.use(plan["out0"][1])
                    for tb in range(NTB):
                        for oc in range(NC_):
                            w, wt = (w0, w0_t) if oc < 4 else (w1, w1_t)
                            bk = kb.bank()
                            pairs = []
                            rd = [wt]
                            for k in range(NC_):
                                if k < 4:
                                    pairs.append((w[:, k, (oc % 4) * P:(oc % 4 + 1) * P], xb[:, k, tbs(tb)]))
                                    rd.append(xb_t[k][tb])
                                else:
                                    pairs.append((w[:, k, (oc % 4) * P:(oc % 4 + 1) * P], cat_hi[:, k - 4, tbs(tb)]))
                                    rd.append(cat_t[k - 4][tb])
                            mm_group(bk, pairs, reads=rd)
                            op("dve", lambda bk=bk, oc=oc, tb=tb: dve.scalar_tensor_tensor(out=xf[:, oc, tbs(tb)], in0=xf[:, oc, tbs(tb)], scalar=DN_ALPHA,
                                                                                        in1=bk[0][:], op0=ALU.mult, op1=ALU.add),
                               reads=[bk[1], xf_t[oc][tb]], writes=[xf_t[oc][tb]])
                        layer_norm_block(tb, "mix_g", "mix_b", 0, tmp)
                    ring.release(plan["out0"][0])
                    ring.release(plan["out0"][1])
            kb.barrier()


        def out_proj_ln(plan_key, src, goff, extra_es=None):
            with ExitStack() as oes:
                tmp = ln_tmps(oes)
                w0, w0_t = ring.use(plan[plan_key][0])
                w1, w1_t = ring.use(plan[plan_key][1])
                for tb in range(NTB):
                    for oc in range(NC_):
                        w, wt = (w0, w0_t) if oc < 4 else (w1, w1_t)
                        bk = kb.bank()
                        pairs = []
                        rd = [wt]
                        for k in range(NC_):
                            a, t = src(k, tb)
                            pairs.append((w[:, k, (oc % 4) * P:(oc % 4 + 1) * P], a))
                            rd.append(t)
                        mm_group(bk, pairs, reads=rd)
                        op("dve", lambda bk=bk, oc=oc, tb=tb: dve.scalar_tensor_tensor(out=xf[:, oc, tbs(tb)], in0=xf[:, oc, tbs(tb)], scalar=DN_ALPHA,
                                                                                    in1=bk[0][:], op0=ALU.mult, op1=ALU.add),
                           reads=[bk[1], xf_t[oc][tb]], writes=[xf_t[oc][tb]])
                    layer_norm_block(tb, "mix_g", "mix_b", goff, tmp)
                ring.release(plan[plan_key][0])
                ring.release(plan[plan_key][1])

        def layer1_mixer():
            lam_init = 0.8 - 0.6 * math.exp(-0.3 * 1)
            with ExitStack() as aes:
                oT = kb.sb("oT", [P, NC_, S], BF16, aes)
                oT_t = [[Tile(f"oT{h}_{t}") for t in range(NTB)] for h in range(NC_)]
                with ExitStack() as hes:
                    cosT = kb.sb("cosT", [P, S], F32, hes)
                    sinT = kb.sb("sinT", [P, S], F32, hes)
                    perm_b = kb.sb("perm_b", [P, P], BF16, hes)
                    subg = kb.sb("subg", [P, P], F32, hes)
                    lamv = kb.sb("lamv", [P, 256], F32, hes)
                    lsc = kb.sb("lsc", [P, 8], F32, hes)
                    ltmp = kb.sb("ltmp", [P, 64], F32, hes)
                    act_c = Tile("attn_consts")
                    perm_t = Tile("perm")
                    dma("sp", cosT[:], cos_d, writes=[act_c], semkey="acst")
                    dma("sp", sinT[:], sin_d, writes=[act_c], semkey="acst")
                    dma("sp", subg[:], subg_d, writes=[act_c], semkey="acst")
                    dma("sp", lamv[:], lamv_d, writes=[act_c], semkey="acst")
                    act_c.w = ("acst", kb.cnt["acst"])
                    dma("pool", perm_b[:], perm_d, writes=[perm_t], semkey="perml")
                    for i in range(2):
                        op("dve", lambda i=i: dve.tensor_tensor(out=ltmp[:], in0=lamv[:, i * 128:i * 128 + 64], in1=lamv[:, i * 128 + 64:i * 128 + 128], op=ALU.mult),
                           reads=[act_c], writes=[act_c])
                        op("dve", lambda i=i: dve.reduce_sum(out=lsc[:, i:i + 1], in_=ltmp[:], axis=mybir.AxisListType.X), reads=[act_c], writes=[act_c])
                    op("act", lambda: act.activation(out=lsc[:, 0:2], in_=lsc[:, 0:2], func=AF.Exp), reads=[act_c], writes=[act_c])
                    op("dve", lambda: dve.tensor_tensor(out=lsc[:, 2:3], in0=lsc[:, 1:2], in1=lsc[:, 0:1], op=ALU.subtract), reads=[act_c], writes=[act_c])
                    op("dve", lambda: dve.tensor_scalar(out=lsc[:, 4:5], in0=lsc[:, 2:3], scalar1=-lam_init, scalar2=None, op0=ALU.add), reads=[act_c], writes=[act_c])
                    neglam = lsc[:, 4:5]

                    qT = (kb.sb("qT", [P, S], BF16, hes), Tile("qT"))
                    kT = (kb.sb("kT", [P, S], BF16, hes), Tile("kT"))
                    vA = (kb.sb("vA", [P, 16, 129], BF16, hes), Tile("vA"))
                    qraw = (kb.sb("qraw", [P, TB], BF16, hes), Tile("qraw"))
                    t1 = (kb.sb("rt1", [P, TB], F32, hes), Tile("rt1"))
                    t2 = (kb.sb("rt2", [P, TB], F32, hes), Tile("rt2"))
                    PT = [(kb.sb(f"PT{i}", [P, TB], BF16, hes), Tile(f"PT{i}")) for i in range(4)]
                    o1s = (kb.sb("o1s", [P, 4, 129], F32, hes), Tile("o1s"))
                    oc_ = (kb.sb("oc", [P, 4, P], F32, hes), Tile("oc"))
                    ob = (kb.sb("ob", [P, 4, P], BF16, hes), Tile("ob"))
                    junk = (kb.sb("junk", [P, P], F32, hes), Tile("junk"))
                    sm = (kb.sb("sm", [P, 16], F32, hes), Tile("sm"))
                    op("pool", lambda: pool.memset(vA[0][:, :, 128:129], 1.0), writes=[vA[1]])
                    pool_o = [kb.banks[0], kb.banks[1]]
                    pool_s = [kb.banks[2], kb.banks[3], kb.banks[4]]
                    pool_g = [kb.banks[5], kb.banks[6], kb.banks[7]]
                    ctr = {"s": 0, "g": 0, "pt": 0}

                    def gbank():
                        b = pool_g[ctr["g"] % 3]
                        ctr["g"] += 1
                        return b

                    for h in range(NC_):
                        g, hc = h // 4, h % 4
                        if hc == 0:
                            (wq, wq_t), (wk, wk_t), (wv, wv_t) = [ring.use(i) for i in plan["qkv"][g]]
                        for (w, wt, dst) in ((wq, wq_t, qT), (wk, wk_t, kT)):
                            for tb in range(NTB):
                                bk = gbank()
                                mm_group(bk, [(w[:, k, hc * P:(hc + 1) * P], xb[:, k, tbs(tb)]) for k in range(NC_)],
                                         reads=[wt] + [xb_t[k][tb] for k in range(NC_)])
                                op("act", lambda bk=bk: act.activation(out=qraw[0][:], in_=bk[0][:], func=AF.Identity), reads=[bk[1]], writes=[qraw[1]])
                                br = gbank()
                                mm_group(br, [(perm_b[:], qraw[0][:])], reads=[perm_t, qraw[1]])
                                op("dve", lambda bk=bk, tb=tb: dve.tensor_tensor(out=t1[0][:], in0=bk[0][:], in1=cosT[:, tbs(tb)], op=ALU.mult),
                                   reads=[bk[1], act_c], writes=[t1[1]])
                                op("dve", lambda br=br, tb=tb: dve.tensor_tensor(out=t2[0][:], in0=br[0][:], in1=sinT[:, tbs(tb)], op=ALU.mult),
                                   reads=[br[1], act_c], writes=[t2[1]])
                                op("pool", lambda dst=dst, tb=tb: pool.tensor_tensor(out=dst[0][:, tbs(tb)], in0=t1[0][:], in1=t2[0][:], op=ALU.add),
                                   reads=[t1[1], t2[1]], writes=[dst[1]])
                        for t4 in range(4):
                            bk = gbank()
                            for j in range(4):
                                tt = t4 * 4 + j
                                for k in range(NC_):
                                    op("pe", lambda bk=bk, j=j, tt=tt, k=k: pe.matmul(bk[0][:, j * P:(j + 1) * P], xb[:, k, tt * P:(tt + 1) * P],
                                                                                  wv[:, k, hc * P:(hc + 1) * P], start=(k == 0), stop=(k == NC_ - 1)),
                                       reads=[wv_t, xb_t[k][tt // 4]], writes=[bk[1]], inc=(j == 3 and k == NC_ - 1))
                            op("act", lambda bk=bk, t4=t4: act.activation(out=vA[0][:, t4 * 4:(t4 + 1) * 4, 0:P],
                                                                        in_=bk[0][:].rearrange("p (a b) -> p a b", a=4), func=AF.Identity),
                               reads=[bk[1]], writes=[vA[1]])
                        if hc == 3:
                            for i in plan["qkv"][g]:
                                ring.release(i)
                        for qb in range(NTB):
                            nkt = 4 * qb + 4
                            for m in range(2):
                                ps_ = slice(m * 64, (m + 1) * 64)
                                oA, oB = pool_o

                                def s_mm(kt):
                                    j0 = max(0, kt - 4 * qb)
                                    ncols = TB - P * j0
                                    sbk = pool_s[ctr["s"] % 3]
                                    ctr["s"] += 1
                                    op("pe", lambda: pe.matmul(sbk[0][:, 0:ncols], kT[0][ps_, kt * P:(kt + 1) * P],
                                                               qT[0][ps_, qb * TB + j0 * P:(qb + 1) * TB], start=True, stop=True),
                                       reads=[kT[1], qT[1]], writes=[sbk[1]])
                                    return sbk, j0, ncols

                                nxt = s_mm(0)
                                for kt in range(nkt):
                                    sbk, j0, ncols = nxt
                                    if kt + 1 < nkt:
                                        nxt = s_mm(kt + 1)
                                    pt = PT[ctr["pt"] % 4]
                                    ctr["pt"] += 1
                                    op("act", lambda: act.activation(out=pt[0][:, 0:ncols], in_=sbk[0][:, 0:ncols], func=AF.Exp, scale=0.125),
                                       reads=[sbk[1]], writes=[pt[1]])
                                    if kt >= 4 * qb:
                                        op("pool", lambda: pool.memset(pt[0][64:128, 0:64], 0.0), reads=[pt[1]], writes=[pt[1]])
                                    for jq in range(j0, 4):
                                        bko = oA if jq < 2 else oB
                                        reg = (jq % 2) * 129
                                        first = (kt == 0 and jq % 2 == 0)
                                        last = (kt == nkt - 1)
                                        op("pe", lambda bko=bko, reg=reg, jq=jq, first=first, last=last:
                                           pe.matmul(bko[0][:, reg:reg + 129], pt[0][:, (jq - j0) * P:(jq - j0 + 1) * P], vA[0][:, kt, :],
                                                     start=first, stop=last, skip_group_check=True),
                                           reads=[pt[1], vA[1]], writes=[bko[1]], inc=(jq == 3))
                                if m == 0:
                                    for i2, bko in enumerate((oA, oB)):
                                        op("dve", lambda i2=i2, bko=bko: dve.tensor_copy(out=o1s[0][:, 2 * i2:2 * i2 + 2, :],
                                                                                      in_=bko[0][:, 0:258].rearrange("p (a b) -> p a b", a=2)),
                                           reads=[bko[1]], writes=[o1s[1]])
                                else:
                                    op("dve", lambda: dve.reciprocal(out=sm[0][:, 0:4], in_=o1s[0][:, :, 128]), reads=[o1s[1]], writes=[sm[1]])
                                    for i2, bko in enumerate((oA, oB)):
                                        op("dve", lambda i2=i2, bko=bko: dve.reciprocal(out=sm[0][:, 4 + 2 * i2:6 + 2 * i2],
                                                                                     in_=bko[0][:, 0:258].rearrange("p (a b) -> p a b", a=2)[:, :, 128]),
                                           reads=[bko[1], sm[1]], writes=[sm[1]])
                                    op("dve", lambda: dve.tensor_scalar(out=sm[0][:, 4:8], in0=sm[0][:, 4:8], scalar1=neglam, scalar2=None, op0=ALU.mult),
                                       reads=[sm[1], act_c], writes=[sm[1]])
                                    for j in range(4):
                                        bko = oA if j < 2 else oB
                                        reg = (j % 2) * 129
                                        op("dve", lambda j=j: dve.tensor_scalar(out=oc_[0][:, j, :], in0=o1s[0][:, j, 0:P], scalar1=sm[0][:, j:j + 1], scalar2=None,
                                                                               op0=ALU.mult), reads=[o1s[1], sm[1]], writes=[oc_[1]])
                                        op("dve", lambda j=j, bko=bko, reg=reg: dve.scalar_tensor_tensor(out=oc_[0][:, j, :], in0=bko[0][:, reg:reg + P], scalar=sm[0][:, 4 + j:5 + j],
                                                                                                     in1=oc_[0][:, j, :], op0=ALU.mult, op1=ALU.add),
                                           reads=[bko[1], sm[1], oc_[1]], writes=[oc_[1]])
                                        op("act", lambda j=j: act.activation(out=junk[0][:], in_=oc_[0][:, j, :], func=AF.Square, accum_out=sm[0][:, 8 + j:9 + j]),
                                           reads=[oc_[1], sm[1]], writes=[junk[1], sm[1]])
                                    op("act", lambda: act.activation(out=sm[0][:, 8:12], in_=sm[0][:, 8:12], func=AF.Sqrt, bias=LN_EPS, scale=1.0 / P),
                                       reads=[sm[1]], writes=[sm[1]])
                                    op("dve", lambda: dve.reciprocal(out=sm[0][:, 12:16], in_=sm[0][:, 8:12]), reads=[sm[1]], writes=[sm[1]])
                                    op("dve", lambda: dve.tensor_scalar(out=sm[0][:, 12:16], in0=sm[0][:, 12:16], scalar1=1.0 - lam_init, scalar2=None, op0=ALU.mult),
                                       reads=[sm[1]], writes=[sm[1]])
                                    for j in range(4):
                                        op("dve", lambda j=j: dve.scalar_tensor_tensor(out=ob[0][:, j, :], in0=oc_[0][:, j, :], scalar=sm[0][:, 12 + j:13 + j], in1=subg[:],
                                                                                       op0=ALU.mult, op1=ALU.mult),
                                           reads=[oc_[1], sm[1], act_c], writes=[ob[1]])
                                    tbk = gbank()
                                    tv = tbk[0][:].bitcast(BF16)
                                    for j in range(4):
                                        op("pe", lambda j=j: pe.transpose(tv[:, j * P:(j + 1) * P], ob[0][:, j, :], ident_b[:]),
                                           reads=[ob[1], cst], writes=[tbk[1]], inc=(j == 3))
                                    op("act", lambda: act.activation(out=oT[:, h, tbs(qb)], in_=tv[:, 0:TB], func=AF.Identity),
                                       reads=[tbk[1]], writes=[oT_t[h][qb]])
                kb.barrier()
                out_proj_ln("out1", lambda k, tb: (oT[:, k, tbs(tb)], oT_t[k][tb]), 8)
            kb.barrier()

        def ffn(l):
            with ExitStack() as fes:
                hid = kb.sb("hid", [P, NFC, 2 * TB], BF16, fes)
                hid_t = [[Tile(f"hid{f}_{t}") for t in range(2)] for f in range(NFC)]
                sl = [(kb.sb(f"sl{i}", [P, TB], F32, fes), Tile(f"sl{i}")) for i in range(3)]
                tmp = ln_tmps(fes)
                sli = 0
                for th in range(2):
                    gu, dn = plan[("ffn", l, th)]
                    for j, ig in enumerate(gu):
                        (wg, wu), wg_t = ring.use(ig)
                        for hc in range(2):
                            f = j * 2 + hc
                            for t2 in range(2):
                                tb = th * 2 + t2
                                xr = [xb_t[k][tb] for k in range(NC_)]
                                bg = kb.bank()
                                mm_group(bg, [(wg[:, k, hc * P:(hc + 1) * P], xb[:, k, tbs(tb)]) for k in range(NC_)], reads=[wg_t] + xr)
                                bu = kb.bank()
                                mm_group(bu, [(wu[:, k, hc * P:(hc + 1) * P], xb[:, k, tbs(tb)]) for k in range(NC_)], reads=[wg_t] + xr)
                                s0, s0t = sl[sli % 3]
                                sli += 1
                                op("act", lambda bg=bg, s0=s0: act.activation(out=s0[:], in_=bg[0][:], func=AF.Silu), reads=[bg[1]], writes=[s0t])
                                op("dve", lambda bu=bu, s0=s0, f=f, t2=t2: dve.tensor_tensor(out=hid[:, f, t2 * TB:(t2 + 1) * TB], in0=bu[0][:], in1=s0[:], op=ALU.mult),
                                   reads=[bu[1], s0t], writes=[hid_t[f][t2]])
                        ring.release(ig)
                    for ch in range(2):
                        bks = [[kb.bank() for t2 in range(2)] for o4 in range(4)]
                        kdone = 0
                        for si in range(3):
                            idx = dn[ch * 3 + si]
                            wd, wd_t = ring.use(idx)
                            nk = 8 if si < 2 else 6
                            for o4 in range(4):
                                for t2 in range(2):
                                    bt, btile = bks[o4][t2]
                                    for kk in range(nk):
                                        f = kdone + kk
                                        first = (f == 0)
                                        last = (f == NFC - 1)
                                        op("pe", lambda bt=bt, wd=wd, kk=kk, o4=o4, f=f, t2=t2, first=first, last=last:
                                           pe.matmul(bt[:, :], wd[:, kk, o4 * P:(o4 + 1) * P], hid[:, f, t2 * TB:(t2 + 1) * TB], start=first, stop=last),
                                           reads=[wd_t, hid_t[f][t2]], writes=[btile], inc=(kk == nk - 1))
                            kdone += nk
                            ring.release(idx)
                        for o4 in range(4):
                            oc = ch * 4 + o4
                            for t2 in range(2):
                                tb = th * 2 + t2
                                bk = bks[o4][t2]
                                op("dve", lambda bk=bk, oc=oc, tb=tb: dve.scalar_tensor_tensor(out=xf[:, oc, tbs(tb)], in0=xf[:, oc, tbs(tb)], scalar=DN_ALPHA,
                                                                                            in1=bk[0][:], op0=ALU.mult, op1=ALU.add),
                                   reads=[bk[1], xf_t[oc][tb]], writes=[xf_t[oc][tb]])
                    for t2 in range(2):
                        layer_norm_block(th * 2 + t2, "ffn_g", "ffn_b", l * 8, tmp)
            kb.barrier()

        for l in layers:
            if stop == "init":
                break
            if l == 0:
                layer0_mixer()
            else:
                layer1_mixer()
            if stop in ("s1", "s2", "mixer"):
                break
            ffn(l)

        for c in range(NC_):
            dma("sp", yT_d[c * P:(c + 1) * P, :], xf[:, c, :], reads=xf_t[c], semkey="out")
        nc.sync.wait_ge(kb.sems["out"], kb.cnt["out"])
        print("instructions:", kb.n_ins, "semaphores:", len(kb.sems))
    return nc


def _prep_inputs(inp):
    f = lambda a: np.ascontiguousarray(np.asarray(a, np.float32))
    pcol = np.zeros((P, NPC), np.float32)

    def put(name, arr):
        a = _cols(arr)
        pcol[:, PC[name]:PC[name] + a.shape[1]] = a
    put("b_in", inp["even_b_in"][0])
    cw = np.asarray(inp["even_conv_w"][0], np.float32)
    a = cw.reshape(31, 4, P).transpose(2, 0, 1).reshape(P, 124)
    pcol[:, PC["conv_w"]:PC["conv_w"] + 124] = a
    put("conv_b", inp["even_conv_b"][0])
    put("cn_g", inp["even_cnorm_g"][0])
    put("cn_b", inp["even_cnorm_b"][0])
    lw = np.asarray(inp["even_lru_conv_w"][0], np.float32)
    pcol[:, PC["lconv_w"]:PC["lconv_w"] + 16] = lw.reshape(4, 4, P).transpose(2, 0, 1).reshape(P, 16)
    put("lconv_b", inp["even_lru_conv_b"][0])
    put("b_a", inp["even_b_a"][0])
    put("b_x", inp["even_b_x"][0])
    put("lam", inp["even_lru_lambda"][0])
    put("mix_g", np.asarray(inp["mix_ln_g"]).reshape(-1))
    put("mix_b", np.asarray(inp["mix_ln_b"]).reshape(-1))
    put("ffn_g", np.asarray(inp["ffn_ln_g"]).reshape(-1))
    put("ffn_b", np.asarray(inp["ffn_ln_b"]).reshape(-1))
    gbd = np.zeros((8, P, P), np.float32)
    for gi_, wname in enumerate(("even_w_a", "even_w_x")):
        w = np.asarray(inp[wname][0], np.float32)
        for c in range(4):
            for b2 in range(2):
                gbd[gi_ * 4 + c, b2 * 64:(b2 + 1) * 64, b2 * 64:(b2 + 1) * 64] = w[2 * c + b2]
    ident = np.eye(P, dtype=np.float32)
    perm = np.zeros((P, P), np.float32)
    for m in range(P):
        d = m % 64
        perm[m + 32 if d < 32 else m - 32, m] = 1.0
    pos = np.arange(S, dtype=np.float32)
    inv_freq = (np.float32(10000.0) ** (-np.arange(0, 64, 2, dtype=np.float32) / np.float32(64))).astype(np.float32)
    ang = (pos[:, None] * inv_freq[None, :]).astype(np.float32)
    cosT = np.zeros((P, S), np.float32)
    sinT = np.zeros((P, S), np.float32)
    for p in range(P):
        d = p % 64
        cosT[p] = np.cos(ang[:, d % 32])
        sinT[p] = np.sin(ang[:, d % 32]) * (-1.0 if d < 32 else 1.0)
    subg = np.ascontiguousarray(np.broadcast_to(np.asarray(inp["odd_subln_g"][0], np.float32)[None, :], (P, P)))
    lamv = np.concatenate([np.asarray(inp[k][0], np.float32) for k in
                           ("odd_lambda_q1", "odd_lambda_k1", "odd_lambda_q2", "odd_lambda_k2")])
    lamv = np.ascontiguousarray(np.broadcast_to(lamv[None, :], (P, 256)))
    shared = {
        "pcol": pcol, "ident": ident, "perm": perm, "cosT": cosT, "sinT": sinT, "gbd": gbd, "subg": subg, "lamv": lamv,
        "w_in": f(inp["even_w_in"][0]), "w_out0": f(inp["even_w_out"][0]),
        "w_qkv": f(inp["odd_w_qkv"][0]), "w_out1": f(inp["odd_w_out"][0]),
    }
    for l in range(2):
        shared[f"wg{l}"] = f(inp["ffn_w_gate"][l])
        shared[f"wu{l}"] = f(inp["ffn_w_up"][l])
        shared[f"wd{l}"] = f(inp["ffn_w_down"][l])
    return shared


def kernel(**inputs):
    x = np.asarray(inputs["x"], np.float32)
    shared = _prep_inputs(inputs)
    nc = build_nc()
    in_maps = []
    for b in range(8):
        m = {k: v for k, v in shared.items() if k in USED_INPUTS}
        m["xT"] = np.ascontiguousarray(x[b].T)
        in_maps.append(m)
    res = run_bass_kernel_spmd(nc, in_maps, core_ids=list(range(8)))
    out = np.stack([np.ascontiguousarray(res.results[b]["yT"].T) for b in range(8)], axis=0)
    return out.astype(np.float32)
```
